# Optimizing a Trainium2 kernel written in Bass

```python
import jax, jax.numpy as jnp
from jax import lax
import numpy as np

D_MODEL = 2048
BATCH = 2
SEQ = 8192
DEPTH = 1

PLE_DIM = 256

ATTN_HEADS = 16
ATTN_KV_HEADS = 4
ATTN_HEAD_DIM = 64
ATTN_GROUP = ATTN_HEADS // ATTN_KV_HEADS
ATTN_Q_WIDTH = ATTN_HEADS * ATTN_HEAD_DIM
ATTN_KV_WIDTH = ATTN_KV_HEADS * ATTN_HEAD_DIM
WINDOW = 128
ATTN_BLOCK = 128
ROPE_THETA = 10000.0

HGRN_WIDTH = D_MODEL // 2
HGRN_EXPAND = 128
HGRN_HEADS = HGRN_WIDTH // HGRN_EXPAND
HGRN_DK = HGRN_EXPAND
HGRN_DV = HGRN_WIDTH // HGRN_HEADS
HGRN_CHUNK = 64

D_FF = ((8 * D_MODEL + 3 * 256 - 1) // (3 * 256)) * 256

RMS_EPS = 1e-6

IN_SIZES = (ATTN_Q_WIDTH, ATTN_KV_WIDTH, ATTN_KV_WIDTH,
            HGRN_HEADS * HGRN_DK, HGRN_HEADS * HGRN_DK, HGRN_HEADS * HGRN_DV, HGRN_HEADS * HGRN_DV,
            D_MODEL, D_MODEL)
IN_WIDTH = sum(IN_SIZES)

kernel_name = "hybrid_swa_sink_hgrn2_gated_block"


def rms_norm(x, gain):
    xf = x.astype(jnp.float32)
    y = xf * lax.rsqrt(jnp.mean(xf * xf, axis=-1, keepdims=True) + RMS_EPS)
    return (y * gain.astype(jnp.float32)).astype(x.dtype)


def rotary(t, positions):
    half = t.shape[-1] // 2
    inv_freq = ROPE_THETA ** (-jnp.arange(half, dtype=jnp.float32) / half)
    ang = positions.astype(jnp.float32)[..., None] * inv_freq
    cos = jnp.cos(ang)[:, :, None, :]
    sin = jnp.sin(ang)[:, :, None, :]
    t1 = t[..., :half].astype(jnp.float32)
    t2 = t[..., half:].astype(jnp.float32)
    return jnp.concatenate([t1 * cos - t2 * sin, t2 * cos + t1 * sin], axis=-1).astype(t.dtype)


def sliding_window_attention(q, k, v, sinks):
    B, S, _, D = q.shape
    nb = S // ATTN_BLOCK
    qb = q.reshape(B, nb, ATTN_BLOCK, ATTN_KV_HEADS, ATTN_GROUP, D)

    def band(t):
        tp = jnp.pad(t, ((0, 0), (ATTN_BLOCK, 0), (0, 0), (0, 0)))
        tb = tp.reshape(B, nb + 1, ATTN_BLOCK, ATTN_KV_HEADS, D)
        return jnp.concatenate([tb[:, :-1], tb[:, 1:]], axis=2)

    kb, vb = band(k), band(v)
    scores = jnp.einsum('bnqhgd,bnkhd->bnhgqk', qb, kb,
                        preferred_element_type=jnp.float32) * (D ** -0.5)
    qi = jnp.arange(ATTN_BLOCK)[:, None] + ATTN_BLOCK
    ki = jnp.arange(2 * ATTN_BLOCK)[None, :]
    local = (ki <= qi) & (qi - ki < WINDOW)
    not_pad = (jnp.arange(nb) > 0)[:, None, None] | (ki >= ATTN_BLOCK)[None]
    valid = local[None] & not_pad
    scores = jnp.where(valid[None, :, None, None], scores, -jnp.inf)
    sink = jnp.broadcast_to(
        sinks.astype(jnp.float32).reshape(1, 1, ATTN_KV_HEADS, ATTN_GROUP, 1, 1),
        scores.shape[:-1] + (1,))
    probs = jax.nn.softmax(jnp.concatenate([scores, sink], axis=-1), axis=-1)[..., :-1]
    out = jnp.einsum('bnhgqk,bnkhd->bnqhgd', probs.astype(v.dtype), vb)
    return out.reshape(B, S, ATTN_HEADS * D)


def hgrn2_recurrence(q, f_logit, i, lb):
    B, S = q.shape[0], q.shape[1]
    nc = S // HGRN_CHUNK
    lb = lb.astype(jnp.float32)
    z = f_logit.astype(jnp.float32)
    log_f = jnp.log(lb + (1.0 - lb) * jax.nn.sigmoid(z))
    k = (1.0 - lb) * jax.nn.sigmoid(-z)
    qf = jax.nn.silu(q.astype(jnp.float32)) * (HGRN_DK ** -0.5)
    vf = i.astype(jnp.float32)

    def to_chunks(t):
        return t.reshape(B, nc, HGRN_CHUNK, HGRN_HEADS, t.shape[-1]).transpose(1, 0, 3, 2, 4)

    qc, kc, vc = to_chunks(qf), to_chunks(k), to_chunks(vf)
    bc = jnp.cumsum(to_chunks(log_f), axis=3)
    causal = jnp.tril(jnp.ones((HGRN_CHUNK, HGRN_CHUNK), dtype=bool))

    def step(state, inp):
        q_c, k_c, v_c, b_c = inp
        b_last = b_c[:, :, -1:, :]
        o_inter = jnp.einsum('bhtk,bhkv->bhtv', q_c * jnp.exp(b_c), state)
        diff = b_c[:, :, :, None, :] - b_c[:, :, None, :, :]
        decay = jnp.exp(jnp.where(causal[:, :, None], diff, -jnp.inf))
        scores = jnp.einsum('bhtk,bhsk,bhtsk->bhts', q_c, k_c, decay)
        o_intra = jnp.einsum('bhts,bhsv->bhtv', scores, v_c)
        new_state = (jnp.exp(b_last[:, :, 0, :])[..., None] * state
                     + jnp.einsum('bhsk,bhsv->bhkv', k_c * jnp.exp(b_last - b_c), v_c))
        return new_state, o_inter + o_intra

    state0 = jnp.zeros((B, HGRN_HEADS, HGRN_DK, HGRN_DV), jnp.float32)
    _, o = lax.scan(step, state0, (qc, kc, vc, bc))
    return o.transpose(1, 0, 3, 2, 4).reshape(B, S, HGRN_HEADS, HGRN_DV).astype(q.dtype)


def setup_inputs(seed: int = 0) -> dict:
    key = jax.random.key(seed)
    ks = jax.random.split(key, 20)

    def dense(k, fan_in, fan_out):
        return jax.random.normal(k, (DEPTH, fan_in, fan_out), jnp.float32) * fan_in ** -0.5

    def gain(k, n):
        return 1.0 + 0.02 * jax.random.normal(k, (DEPTH, n), jnp.float32)

    return {
        "x": jax.random.normal(ks[0], (BATCH, SEQ, D_MODEL), jnp.float32),
        "p": jax.random.normal(ks[1], (DEPTH, BATCH, SEQ, PLE_DIM), jnp.float32),
        "positions": jnp.broadcast_to(jnp.arange(SEQ, dtype=jnp.int32), (BATCH, SEQ)),
        "g_mix_pre": gain(ks[2], D_MODEL),
        "w_in": dense(ks[3], D_MODEL, IN_WIDTH),
        "attn_sinks": jax.random.normal(ks[4], (DEPTH, ATTN_HEADS), jnp.float32),
        "hgrn_lb_logits": 0.5 * jax.random.normal(ks[5], (DEPTH + 1, HGRN_HEADS * HGRN_DK), jnp.float32),
        "hgrn_gnorm": gain(ks[6], HGRN_DV),
        "w_attn_branch": dense(ks[7], ATTN_Q_WIDTH, D_MODEL),
        "w_hgrn_branch": dense(ks[8], HGRN_HEADS * HGRN_DV, D_MODEL),
        "w_out": dense(ks[9], D_MODEL, D_MODEL),
        "g_mix_post": gain(ks[10], D_MODEL),
        "g_ffn_pre": gain(ks[11], D_MODEL),
        "w_gate_up": dense(ks[12], D_MODEL, 2 * D_FF),
        "w_down": dense(ks[13], D_FF, D_MODEL),
        "g_ffn_post": gain(ks[14], D_MODEL),
        "g_ple_pre": gain(ks[15], D_MODEL),
        "w_ple_gate": dense(ks[16], D_MODEL, D_MODEL),
        "w_ple_proj": dense(ks[17], PLE_DIM, D_MODEL),
        "g_ple_post": gain(ks[18], D_MODEL),
    }


def reference(x, p, positions, g_mix_pre, w_in, attn_sinks, hgrn_lb_logits, hgrn_gnorm,
              w_attn_branch, w_hgrn_branch, w_out, g_mix_post, g_ffn_pre, w_gate_up, w_down,
              g_ffn_post, g_ple_pre, w_ple_gate, w_ple_proj, g_ple_post):
    B, S, _ = x.shape
    split_at = np.cumsum(IN_SIZES)[:-1].tolist()
    lb_all = jnp.cumsum(jax.nn.softmax(hgrn_lb_logits.astype(jnp.float32), axis=0), axis=0)
    for layer in range(DEPTH):
        h = rms_norm(x, g_mix_pre[layer])
        proj = h @ w_in[layer]
        aq, ak, av, hq, hf, hi, hg, gate_a, gate_b = jnp.split(proj, split_at, axis=-1)

        aq = rotary(aq.reshape(B, S, ATTN_HEADS, ATTN_HEAD_DIM), positions)
        ak = rotary(ak.reshape(B, S, ATTN_KV_HEADS, ATTN_HEAD_DIM), positions)
        av = av.reshape(B, S, ATTN_KV_HEADS, ATTN_HEAD_DIM)
        y_attn = sliding_window_attention(aq, ak, av, attn_sinks[layer])

        o = hgrn2_recurrence(hq.reshape(B, S, HGRN_HEADS, HGRN_DK),
                             hf.reshape(B, S, HGRN_HEADS, HGRN_DK),
                             hi.reshape(B, S, HGRN_HEADS, HGRN_DV),
                             lb_all[layer].reshape(HGRN_HEADS, HGRN_DK))
        o = rms_norm(o, hgrn_gnorm[layer]) * jax.nn.silu(hg.reshape(B, S, HGRN_HEADS, HGRN_DV))
        y_hgrn = o.reshape(B, S, HGRN_HEADS * HGRN_DV)

        merged = (jax.nn.sigmoid(gate_a) * (y_attn @ w_attn_branch[layer])
                  + jax.nn.sigmoid(gate_b) * (y_hgrn @ w_hgrn_branch[layer]))
        x = x + rms_norm(merged @ w_out[layer], g_mix_post[layer])

        h = rms_norm(x, g_ffn_pre[layer])
        gate, up = jnp.split(h @ w_gate_up[layer], 2, axis=-1)
        x = x + rms_norm((jax.nn.silu(gate) * up) @ w_down[layer], g_ffn_post[layer])

        ple_gate = jax.nn.sigmoid(rms_norm(x, g_ple_pre[layer]) @ w_ple_gate[layer])
        e = (p[layer].astype(x.dtype) @ w_ple_proj[layer]) * ple_gate
        x = x + rms_norm(e, g_ple_post[layer])
    return x
```

```python
import math
from contextlib import ExitStack
import numpy as np
import concourse.bass as bass
import concourse.mybir as mybir
from concourse.bass_utils import run_bass_kernel_spmd

F32 = mybir.dt.float32
BF16 = mybir.dt.bfloat16
I32 = mybir.dt.int32
U8 = mybir.dt.uint8
AF = mybir.ActivationFunctionType
ALU = mybir.AluOpType

D = 2048
KC = 16
T = 1024
NT = 8
TH = 1152
NTH = 9
SEG = 2048
DFF = 5632
IN_W = 9728
OFF_AQ, OFF_AK, OFF_AV, OFF_HQ, OFF_HF, OFF_HI, OFF_HG, OFF_GA, OFF_GB = 0, 1024, 1280, 1536, 2560, 3584, 4608, 5632, 7680
EPS = 1e-6
NCORES = 8
GROWS = 1152
TWO_PI = 2.0 * math.pi
CW1 = 6.28125
CW2 = TWO_PI - CW1


class Buf:
    __slots__ = ("name", "w", "r")

    def __init__(self, name=""):
        self.name = name
        self.w = None
        self.r = {}


class Prog:
    ENG = ("pe", "act", "dve", "pool", "sp")

    def __init__(self, nc, es):
        self.nc = nc
        self.es = es
        self.q = {e: [] for e in self.ENG}
        self.semh = {}
        self.cnt = {e: 0 for e in self.ENG}
        self.seen = {e: {} for e in self.ENG}
        self.dcnt = {}
        self.bar = {e: [] for e in self.ENG}
        for e in self.ENG:
            self.semh["e_" + e] = es.enter_context(nc.semaphore("s_" + e))

    def dsem(self, name):
        k = "d_" + name
        if k not in self.semh:
            self.semh[k] = self.es.enter_context(self.nc.semaphore("s_" + k))
            self.dcnt[k] = 0
        return k

    def _deps(self, eng, reads, writes):
        deps = {}

        def add(tok):
            if tok is None:
                return
            k, v = tok
            if deps.get(k, 0) < v:
                deps[k] = v
        for b in reads:
            add(b.w)
        for b in writes:
            add(b.w)
            for k, v in b.r.items():
                add((k, v))
        for tok in self.bar[eng]:
            add(tok)
        self.bar[eng] = []
        waits = []
        mykey = "e_" + eng
        for k, v in deps.items():
            if k == mykey and eng == "pe":
                continue
            if self.seen[eng].get(k, 0) < v:
                self.seen[eng][k] = v
                waits.append((k, v))
        return waits

    def _mark(self, tok, reads, writes):
        for b in reads:
            if b.r.get(tok[0], 0) < tok[1]:
                b.r[tok[0]] = tok[1]
        for b in writes:
            b.w = tok
            b.r = {}

    def op(self, eng, fn, reads=(), writes=(), sig=True):
        waits = self._deps(eng, reads, writes)
        if sig:
            self.cnt[eng] += 1
        tok = ("e_" + eng, self.cnt[eng] if sig else self.cnt[eng] + 1)
        self._mark(tok, reads, writes)
        self.q[eng].append((waits, fn, True if sig else None))
        return tok

    SW_RING_LIMIT = 640

    def dma(self, queue, out, in_, reads, writes, sem):
        waits = self._deps(queue, reads, writes)
        if queue == "pool":
            shp = list(out.shape)
            nd = max(1, int(np.prod(shp[:-1])) // 16)
            infl = self.__dict__.setdefault("infl", [])
            while infl and sum(x[2] for x in infl) + nd > self.SW_RING_LIMIT:
                k = infl[0][0]
                v = max(x[1] for x in infl if x[0] == k)
                if self.seen[queue].get(k, 0) < v:
                    self.seen[queue][k] = v
                    waits.append((k, v))
                self.infl = infl = [x for x in infl if x[0] != k]
            infl.append((sem, self.dcnt[sem] + 16, nd))
        self.dcnt[sem] += 16
        tok = (sem, self.dcnt[sem])
        self._mark(tok, reads, writes)
        self.q[queue].append((waits, lambda e: e.dma_start(out=out, in_=in_), sem))
        return tok

    def barrier(self):
        toks = [("e_" + e, self.cnt[e]) for e in self.ENG if self.cnt[e] > 0]
        toks += [(k, v) for k, v in self.dcnt.items() if v > 0]
        for e in self.ENG:
            self.bar[e] = list(toks)

    def replay(self, block):
        def mk(name):
            def body(e):
                for waits, fn, sig in self.q[name]:
                    for k, v in waits:
                        e.wait_ge(self.semh[k], v)
                    ins = fn(e)
                    if sig is True:
                        ins.then_inc(self.semh["e_" + name], 1)
                    elif isinstance(sig, tuple):
                        ins.then_inc(self.semh[sig[1]])
                    elif sig is not None:
                        ins.then_inc(self.semh[sig], 16)
                if name == "sp":
                    for k, v in self.dcnt.items():
                        if v > 0:
                            e.wait_ge(self.semh[k], v)
            return body
        block.tensor(mk("pe"))
        block.scalar(mk("act"))
        block.vector(mk("dve"))
        block.gpsimd(mk("pool"))
        block.sync(mk("sp"))


class Arena:
    def __init__(self, ap):
        self.ap = ap

    def view(self, off, dt, shape):
        esz = {F32: 4, BF16: 2, I32: 4}[dt]
        n = int(np.prod(shape[1:]))
        v = self.ap[:, off:off + n * esz].bitcast(dt)
        if len(shape) == 3:
            v = v.rearrange("p (a b) -> p a b", a=shape[1])
        elif len(shape) == 4:
            v = v.rearrange("p (a b c) -> p a b c", a=shape[1], b=shape[2])
        return v


def build_program(stop=None, skipx=False, var=0):
    nc = bass.Bass("TRN2", target_bir_lowering=False)
    es = ExitStack()

    def din(name, shape, dt=F32):
        return nc.dram_tensor(name, list(shape), dt, kind="ExternalInput").ap()

    xh = din("xh", [SEG + 128, D])
    pin = din("p", [SEG, 256])
    posr = din("posr", [128, SEG + 128], I32)
    c_invf = din("c_invf", [128, 1])
    c_sgn = din("c_sgn", [128, 1])
    c_ident = din("c_ident", [128, 128])
    c_ones = din("c_ones", [128, 128])
    c_perm = din("c_perm", [128, 128])
    c_mcur = din("c_mcur", [128, 512])
    c_mprevA = din("c_mprevA", [128, 512])
    c_mprevB = din("c_mprevB", [128, 512])
    c_caus = din("c_caus", [128, 128])
    c_sel = din("c_sel", [128, NCORES])
    sink_rep = din("sink_rep", [128, 16])
    lbl = din("lbl", [128, 16])
    gnc = din("gnc", [128, 1])
    g_mix_pre = din("g_mix_pre", [128, D])
    g_mix_post = din("g_mix_post", [128, D])
    g_ffn_pre = din("g_ffn_pre", [128, D])
    g_ffn_post = din("g_ffn_post", [128, D])
    g_ple_pre = din("g_ple_pre", [128, D])
    g_ple_post = din("g_ple_post", [128, D])
    w_in = din("w_in", [D, IN_W])
    w_ab = din("w_ab", [1024, D])
    w_hb = din("w_hb", [1024, D])
    w_out = din("w_out", [D, D])
    w_gu = din("w_gu", [D, 2 * DFF])
    w_down = din("w_down", [DFF, D])
    w_pg = din("w_pg", [D, D])
    w_pp = din("w_pp", [256, D])
    yout = nc.dram_tensor("y", [SEG, D], F32, kind="ExternalOutput").ap()
    x1d = nc.dram_tensor("x1d", [SEG, D], F32).ap()
    x2d = nc.dram_tensor("x2d", [SEG, D], F32).ap()
    hTc = nc.dram_tensor("hTc", [2, 128, KC, T], BF16).ap()
    vc = nc.dram_tensor("vc", [2, 128, NT, 1024], BF16).ap()
    kc_ktb = nc.dram_tensor("kc_ktb", [2, 8, 128, T], BF16).ap()
    kc_kht = nc.dram_tensor("kc_kht", [2, 8, 128, T], BF16).ap()
    kc_p = nc.dram_tensor("kc_p", [2, 8, 128, T], F32).ap()
    bnc = nc.dram_tensor("bnc", [GROWS, 128], F32)
    gth = nc.dram_tensor("gth", [NCORES * GROWS, 128], F32)

    ARENA_BYTES = 212480
    arena_t = es.enter_context(nc.sbuf_tensor("arena", [128, ARENA_BYTES], U8))
    A = Arena(arena_t)
    banks = [es.enter_context(nc.psum_tensor("ps%d" % i, [128, 512], F32)) for i in range(8)]
    P = Prog(nc, es)

    off = [0]

    def alloc(nbytes):
        o = off[0]
        off[0] += (nbytes + 31) // 32 * 32
        return o

    ident = A.view(alloc(256), BF16, [128, 128])
    ones = A.view(alloc(256), BF16, [128, 128])
    perm = A.view(alloc(256), BF16, [128, 128])
    caus = A.view(alloc(256), BF16, [128, 128])
    mcur = A.view(alloc(1024), BF16, [128, 512])
    mprevA = A.view(alloc(1024), BF16, [128, 512])
    mprevB = A.view(alloc(1024), BF16, [128, 512])
    zer = A.view(alloc(256), F32, [128, 64])
    smallf = A.view(alloc(4 * 128), F32, [128, 128])
    invf = smallf[:, 0:1]
    sgn = smallf[:, 1:2]
    gn05 = smallf[:, 2:3]
    epsc = smallf[:, 3:4]
    sel = smallf[:, 8:16]
    se = smallf[:, 16:32]
    lb0 = smallf[:, 32:40]
    lb1 = smallf[:, 40:48]
    c1 = smallf[:, 48:56]
    c2 = smallf[:, 56:64]
    c3 = smallf[:, 64:72]
    dtot = smallf[:, 72:80]
    ssn = smallf[:, 80:89]
    rsn = smallf[:, 89:98]
    ssr = smallf[:, 98:106]
    rsr = smallf[:, 106:114]
    dsel = smallf[:, 114:122]
    cosT = A.view(alloc(4 * TH), F32, [128, TH])
    sinT = A.view(alloc(4 * TH), F32, [128, TH])
    Sst = A.view(alloc(4 * 1024), F32, [128, 8, 128])
    CONST_END = off[0]
    HT = A.view(alloc(2 * KC * TH), BF16, [128, KC, TH])
    O2_OFF = alloc(4 * NT * D)
    OUT2 = A.view(O2_OFF, F32, [128, NT, D])
    SLAB_OFF = [alloc(16384), alloc(16384)]
    R_OFF = off[0]
    R_SIZE = ARENA_BYTES - R_OFF
    assert R_SIZE >= 57856, R_SIZE

    ro = [R_OFF]

    def ralloc(n):
        o = ro[0]
        ro[0] += (n + 31) // 32 * 32
        assert ro[0] <= ARENA_BYTES, "R overflow"
        return o
    MERGED = A.view(ralloc(2 * KC * T), BF16, [128, KC, T])
    YTH = A.view(ralloc(2 * 8 * T), BF16, [128, 8, T])
    VTM = A.view(ralloc(2 * NTH * 256), BF16, [128, NTH, 256])
    ro[0] = R_OFF
    XT = [A.view(ralloc(4 * D), F32, [128, D]) for _ in range(2)]
    XNB = [A.view(ralloc(2 * D), BF16, [128, D])] * 2
    JUNK = XNB[1]
    g_off = ralloc(4 * D)
    GAIN = [A.view(g_off, F32, [128, D]), A.view(g_off, F32, [128, D])]
    PT = A.view(g_off, BF16, [128, 2, T])
    HID = A.view(ralloc(2 * 11 * T), BF16, [128, 11, T])
    WPP = HID[:, 0:8, :].rearrange("p a b -> p (a b)")[:, 0:4096].rearrange("p (k n) -> p k n", k=2)
    assert ro[0] <= ARENA_BYTES - 4096
    M_T = A.view(ARENA_BYTES - 4096, F32, [128, 512])
    M_M2 = A.view(ARENA_BYTES - 2048, F32, [128, 512])

    oo = [O2_OFF]

    def oalloc(n):
        o = oo[0]
        oo[0] += (n + 31) // 32 * 32
        assert oo[0] <= O2_OFF + 4 * NT * D, "O2 overflow"
        return o
    KT = A.view(oalloc(2 * 2 * TH), BF16, [128, 2, TH])
    QT = A.view(oalloc(2 * 4 * T), BF16, [128, 4, T])
    YTA = A.view(oalloc(2 * 8 * T), BF16, [128, 8, T])
    ESB = [A.view(oalloc(1024), BF16, [128, 512]) for _ in range(2)]
    PM = [A.view(oalloc(1024), BF16, [128, 512]) for _ in range(2)]
    REC = A.view(oalloc(2048), F32, [128, 512])
    QB = A.view(oalloc(1024), BF16, [128, 512])
    RT1 = A.view(oalloc(2048), F32, [128, 512])
    RT2 = A.view(oalloc(2048), F32, [128, 512])
    oo[0] = O2_OFF
    H_V = A.view(oalloc(2 * NT * 1024), BF16, [128, NT, 1024])
    H_SQ = A.view(oalloc(1024), BF16, [128, 512])
    H_OSB = A.view(oalloc(2048), F32, [128, 512])
    H_RS = A.view(oalloc(2048), F32, [128, 512])
    H_SB = A.view(oalloc(1024), BF16, [128, 4, 128])
    H_AM = A.view(oalloc(512), BF16, [128, 2, 128])
    H_GU = A.view(oalloc(4 * 1024), F32, [128, 8, 128])
    H_GD = A.view(oalloc(4 * 128), F32, [128, 128])
    H_PL = A.view(oalloc(128), F32, [128, 2, 16])
    H_KHT = [A.view(oalloc(2 * T), BF16, [128, NT, 128]) for _ in range(2)]
    H_TH = A.view(oalloc(4 * T), F32, [128, T])
    H_P = A.view(oalloc(4 * T), F32, [128, T])
    hbase = oo[0]
    H_KK = A.view(oalloc(4 * T), F32, [128, T])
    H_FF = A.view(oalloc(4 * T), F32, [128, T])
    H_KHB = A.view(oalloc(2 * T), BF16, [128, T])
    H_KTB0 = A.view(oalloc(2 * T), BF16, [128, T])
    oo[0] = hbase
    H_U = A.view(oalloc(4 * T), F32, [128, T])
    H_G2 = [A.view(oalloc(4 * T), F32, [128, T]) for _ in range(2)]
    H_QTB = [A.view(oalloc(2 * T), BF16, [128, T]) for _ in range(2)]
    H_KTB = [A.view(oalloc(2 * T), BF16, [128, T]) for _ in range(2)]

    b_bank = [Buf("bank%d" % i) for i in range(8)]
    b_slab = [Buf("slab0"), Buf("slab1")]
    b_const = Buf("const")
    b_ht = [Buf("ht%d" % i) for i in range(NTH)]
    b_xt = [Buf("xt0"), Buf("xt1")]
    b_xnb = [Buf("xnb0")] * 2
    b_gain = [Buf("gain0")] * 2
    b_junk = b_xnb[1]
    b_ss = Buf("ss")
    b_o2 = [Buf("o2_%d" % i) for i in range(NT)]
    b_misc = {}

    def B(name):
        if name not in b_misc:
            b_misc[name] = Buf(name)
        return b_misc[name]

    bank_i = [0]

    def nbank():
        i = bank_i[0] % 6
        bank_i[0] += 1
        return banks[i], b_bank[i]

    hold_i = [0]

    def hbank():
        i = 6 + hold_i[0] % 2
        hold_i[0] += 1
        return banks[i], b_bank[i]

    slab_i = [0]
    sem_slab = [P.dsem("slab0"), P.dsem("slab1")]

    def nslab():
        i = slab_i[0] % 2
        slab_i[0] += 1
        return i

    sem_setup = P.dsem("setup")
    sem_xt = [P.dsem("xt0"), P.dsem("xt1")]
    sem_st = [P.dsem("st0"), P.dsem("st1")]
    sem_gain = [P.dsem("gain0")] * 2
    sem_misc = P.dsem("misc")
    sem_pmisc = P.dsem("pmisc")
    sem_cst = P.dsem("cst")
    sem_cld = {k: P.dsem("cld_" + k) for k in ("ht0", "ht", "v", "ktb0", "ktb1", "kht0", "kht1", "p")}

    def slabview(i, part, a, b):
        return A.view(SLAB_OFF[i], BF16, [128, a, b])[:part]

    def wview(w, r0, nk, c0, ncol, part=128):
        return w[r0:r0 + nk * part, c0:c0 + ncol].rearrange("(k p) n -> p k n", p=part)

    def act(out, in_, func, reads, writes, bias=None, scale=None, accum_out=None):
        kw = {}
        if bias is not None:
            kw["bias"] = bias
        if scale is not None:
            kw["scale"] = scale
        if accum_out is not None:
            kw["accum_out"] = accum_out
        return P.op("act", lambda e: e.activation(out=out, in_=in_, func=func, **kw), reads, writes)

    def ts(out, in0, s1, s2, op0, op1, reads, writes, eng="dve"):
        if op1 is None:
            return P.op(eng, lambda e: e.tensor_scalar(out=out, in0=in0, scalar1=s1, scalar2=None, op0=op0), reads, writes)
        return P.op(eng, lambda e: e.tensor_scalar(out=out, in0=in0, scalar1=s1, scalar2=s2, op0=op0, op1=op1), reads, writes)

    def tt(out, in0, in1, op, reads, writes, eng="dve"):
        return P.op(eng, lambda e: e.tensor_tensor(out=out, in0=in0, in1=in1, op=op), reads, writes)

    def stt(out, in0, scalar, in1, op0, op1, reads, writes):
        return P.op("dve", lambda e: e.scalar_tensor_tensor(out=out, in0=in0, scalar=scalar, in1=in1, op0=op0, op1=op1), reads, writes)

    def cp(out, in_, reads, writes, eng="dve"):
        if eng == "act":
            return P.op(eng, lambda e: e.activation(out=out, in_=in_, func=AF.Copy), reads, writes)
        return P.op(eng, lambda e: e.tensor_copy(out=out, in_=in_), reads, writes)

    def recip(out, in_, reads, writes):
        return P.op("dve", lambda e: e.reciprocal(out=out, in_=in_), reads, writes)

    def mm(out, lhsT, rhs, start, stop, reads, writes, sig=None):
        if sig is None:
            sig = stop
        return P.op("pe", lambda e: e.matmul(out, lhsT, rhs, start=start, stop=stop), reads, writes, sig=sig)

    def tr(out, in_, idn, reads, writes, sig):
        return P.op("pe", lambda e: e.transpose(out, in_, idn), reads, writes, sig=sig)

    def setup():
        tmp = A.view(O2_OFF, F32, [128, 2048])
        bt = B("setup_tmp")
        col = [0]

        def ld_cast(dst, src, npart, ncol):
            c = col[0]
            col[0] += ncol
            P.dma("sp", tmp[:npart, c:c + ncol], src, [], [bt], sem_setup)
            cp(dst, tmp[:npart, c:c + ncol], [bt], [b_const])
        ld_cast(ident, c_ident, 128, 128)
        ld_cast(ones, c_ones, 128, 128)
        ld_cast(perm, c_perm, 128, 128)
        ld_cast(caus, c_caus, 128, 128)
        ld_cast(mcur, c_mcur, 128, 512)
        ld_cast(mprevA, c_mprevA, 128, 512)
        ld_cast(mprevB, c_mprevB, 128, 512)
        P.dma("sp", invf, c_invf, [], [b_const], sem_setup)
        P.dma("sp", sgn, c_sgn, [], [b_const], sem_setup)
        P.dma("sp", sel, c_sel, [], [b_const], sem_setup)
        P.dma("sp", se, sink_rep, [], [b_const], sem_setup)
        P.dma("sp", lb0, lbl[:, 0:8], [], [b_const], sem_setup)
        P.dma("sp", lb1, lbl[:, 8:16], [], [b_const], sem_setup)
        P.dma("sp", gn05, gnc, [], [b_const], sem_setup)
        P.op("dve", lambda e: e.memset(zer, 0.0), [], [b_const])
        P.op("dve", lambda e: e.memset(epsc, EPS), [], [b_const])
        P.op("dve", lambda e: e.memset(Sst, 0.0), [], [B("S")])
        P.op("dve", lambda e: e.memset(dtot, 1.0), [], [B("dtot")])
        ts(gn05, gn05, 0.5, None, ALU.mult, None, [b_const], [b_const])
        act(se, se, AF.Exp, [b_const], [b_const])
        tt(lb0, lb0, lb1, ALU.subtract, [b_const], [b_const])
        act(lb0, lb0, AF.Tanh, [b_const], [b_const], scale=0.5)
        ts(c2, lb0, -0.25, 0.25, ALU.mult, ALU.add, [b_const], [b_const])
        ts(c1, c2, -1.0, None, ALU.mult, None, [b_const], [b_const])
        ts(c3, c2, -1.0, 1.0, ALU.mult, ALU.add, [b_const], [b_const])
        ts(dsel, sel, -1.0, 1.0, ALU.mult, ALU.add, [b_const], [b_const])

    def rope_tables(hf):
        bt = B("rope_tmp")
        pi32 = A.view(O2_OFF, I32, [128, TH])
        ang = A.view(O2_OFF + 4 * TH, F32, [128, TH])
        rr = A.view(O2_OFF + 8 * TH, F32, [128, TH])
        ri = A.view(O2_OFF + 12 * TH, I32, [128, TH])
        t0 = hf * T
        P.dma("sp", pi32, posr[:, t0:t0 + TH], [], [bt], sem_misc)
        cp(ang, pi32, [bt], [bt])
        ts(ang, ang, invf, None, ALU.mult, None, [bt, b_const], [bt])
        for dst, shift in ((sinT, 0.0), (cosT, math.pi / 2)):
            ts(rr, ang, shift, 1.0 / TWO_PI, ALU.add, ALU.mult, [bt], [bt])
            cp(ri, rr, [bt], [bt])
            cp(rr, ri, [bt], [bt])
            bd = B("ropedst")
            stt(dst, rr, -CW1, ang, ALU.mult, ALU.add, [bt], [bd])
            stt(dst, rr, -CW2, dst, ALU.mult, ALU.add, [bt, bd], [bd])
            if shift != 0.0:
                ts(dst, dst, shift, None, ALU.add, None, [bd], [bd])
            ts(rr, dst, math.pi, -TWO_PI, ALU.is_gt, ALU.mult, [bd], [bt])
            tt(dst, dst, rr, ALU.add, [bd, bt], [bd])
            ts(rr, dst, -math.pi, TWO_PI, ALU.is_lt, ALU.mult, [bd], [bt])
            tt(dst, dst, rr, ALU.add, [bd, bt], [bd])
            ts(dst, dst, 3.1415925, -3.1415925, ALU.min, ALU.max, [bd], [bd])
            act(dst, dst, AF.Sin, [bd], [bd])
        ts(sinT, sinT, sgn, None, ALU.mult, None, [B("ropedst"), b_const], [B("ropedst")])

    def norm_T(src_rows, ntiles, gain_ap, dstT, dst_bufs, tile0=0):
        gi = 0
        P.dma("sp", GAIN[gi], gain_ap, [], [b_gain[gi]], sem_gain[gi])
        for i in range(ntiles):
            s = i % 2
            P.dma("sp", XT[s], src_rows(i), [], [b_xt[s]], sem_xt[s])
            act(JUNK, XT[s], AF.Square, [b_xt[s]], [b_junk, b_ss], accum_out=ssn[:, i:i + 1])
        act(rsn[:, 0:ntiles], ssn[:, 0:ntiles], AF.Sqrt, [b_ss, b_const], [b_ss], bias=epsc, scale=1.0 / D)
        recip(rsn[:, 0:ntiles], rsn[:, 0:ntiles], [b_ss], [b_ss])
        for i in range(ntiles):
            s = i % 2
            P.dma("sp", XT[s], src_rows(i), [], [b_xt[s]], sem_xt[s])
            stt(XNB[s], XT[s], rsn[:, i:i + 1], GAIN[gi], ALU.mult, ALU.mult, [b_xt[s], b_ss, b_gain[gi]], [b_xnb[s]])
            for half in range(2):
                bk, bb = nbank()
                bkb = bk[:, :].bitcast(BF16).rearrange("p (a b) -> p a b", a=8)
                for j in range(8):
                    kc = half * 8 + j
                    tr(bkb[:, j, :], XNB[s][:, kc * 128:(kc + 1) * 128], ident, [b_xnb[s], b_const], [bb], sig=(j == 7))
                c0 = (tile0 + i) * 128
                cp(dstT[:, half * 8:(half + 1) * 8, c0:c0 + 128], bkb, [bb], [dst_bufs[tile0 + i]], eng="act")

    def load_slab(pieces):
        si = nslab()
        for dst, src in pieces:
            P.dma("pool", dst, src, [], [b_slab[si]], sem_slab[si])
        return si

    def lin_tm(actT, act_bufs, nk, wsrc, evac, kpart=128):
        for cb in range(4):
            si = nslab()
            sv = slabview(si, kpart, nk, 512)
            P.dma("pool", sv, wsrc(cb), [], [b_slab[si]], sem_slab[si])
            for i in range(NT):
                bk, bb = nbank()
                for k in range(nk):
                    mm(bk[:, :], actT[:kpart, k, i * 128:(i + 1) * 128], sv[:, k, :], k == 0, k == nk - 1,
                       [b_slab[si]] + act_bufs(i), [bb])
                evac(cb, i, bk, bb)

    def rope_evac(bk, bb, dst, dst_buf, tok0, n):
        bq = B("ropeq")
        cp(QB[:, :n], bk[:, :n], [bb], [bq], eng="act")
        bk2, bb2 = nbank()
        mm(bk2[:, :n], perm, QB[:, :n], True, True, [bq, b_const], [bb2])
        tt(RT1[:, :n], bk[:, :n], cosT[:, tok0:tok0 + n], ALU.mult, [bb, bq, B("ropedst")], [B("rt1")])
        tt(RT2[:, :n], bk2[:, :n], sinT[:, tok0:tok0 + n], ALU.mult, [bb2, B("ropedst")], [B("rt2")])
        tt(dst, RT1[:, :n], RT2[:, :n], ALU.add, [B("rt1"), B("rt2")], [dst_buf])

    def attention(hf):
        si = nslab()
        sv = slabview(si, 128, KC, 512)
        P.dma("pool", sv, wview(w_in, 0, KC, OFF_AK, 512), [], [b_slab[si]], sem_slab[si])
        bkt = B("kt")
        for gp in range(2):
            for (t0, n) in ((0, 512), (512, 512), (1024, 128)):
                bk, bb = nbank()
                for k in range(KC):
                    mm(bk[:, :n], sv[:, k, gp * 128:(gp + 1) * 128], HT[:, k, t0:t0 + n], k == 0, k == KC - 1,
                       [b_slab[si]] + b_ht, [bb])
                rope_evac(bk, bb, KT[:, gp, t0:t0 + n], bkt, t0, n)
        bv = B("vtm")
        for i in range(NTH):
            bk, bb = nbank()
            for k in range(KC):
                mm(bk[:, :256], HT[:, k, i * 128:(i + 1) * 128], sv[:, k, 256:512], k == 0, k == KC - 1,
                   [b_slab[si], b_ht[i]], [bb])
            cp(VTM[:, i, :], bk[:, :256], [bb], [bv], eng="act")
        bqt = B("qt")
        byta = B("yta")
        if stop == "attn_k":
            return
        for gp in range(2):
            si = nslab()
            sq = A.view(SLAB_OFF[si], BF16, [128, KC, 4, 128])
            for o in range(2):
                g = 2 * gp + o
                for i in range(4):
                    P.dma("pool", sq[:, :, i, o * 64:(o + 1) * 64], wview(w_in, 0, KC, OFF_AQ + g * 256 + i * 64, 64),
                          [], [b_slab[si]], sem_slab[si])
            for i in range(4):
                for tb in range(2):
                    bk, bb = nbank()
                    t0 = 128 + tb * 512
                    for k in range(KC):
                        mm(bk[:, :], sq[:, k, i, :], HT[:, k, t0:t0 + 512], k == 0, k == KC - 1,
                           [b_slab[si]] + b_ht, [bb])
                    rope_evac(bk, bb, QT[:, i, tb * 512:(tb + 1) * 512], bqt, t0, 512)
            if stop == "attn_q":
                return
            for o in range(2):
                g = 2 * gp + o
                rows = slice(o * 64, (o + 1) * 64)
                for qt in range(NT):
                    bky, bby = nbank()
                    bkd, bbd = nbank()
                    for kb in range(2):
                        kt0 = (qt + kb) * 128
                        bks, bbs = nbank()
                        mm(bks[:, :].rearrange("p (a b) -> p a b", a=4), KT[rows, gp, kt0:kt0 + 128],
                           QT[rows, :, qt * 128:(qt + 1) * 128], True, True, [bkt, bqt], [bbs])
                        be = B("esb%d" % kb)
                        act(ESB[kb], bks[:, :], AF.Exp, [bbs], [be], scale=0.125)
                        if kb == 1:
                            msk = mcur
                        else:
                            msk = mprevA if (hf == 0 and qt == 0) else mprevB
                        bp = B("pm%d" % kb)
                        tt(PM[kb], ESB[kb], msk, ALU.mult, [be, b_const], [bp])
                        mm(bky[:, :], VTM[:, qt + kb, gp * 128:(gp + 1) * 128], PM[kb], kb == 0, kb == 1, [bv, bp], [bby])
                        mm(bkd[:, :], ones, PM[kb], kb == 0, kb == 1, [b_const, bp], [bbd])
                    brec = B("rec")
                    for i in range(4):
                        h = 4 * g + i
                        ts(REC[rows, i * 128:(i + 1) * 128], bkd[rows, i * 128:(i + 1) * 128], se[rows, h:h + 1], None,
                           ALU.add, None, [bbd, b_const], [brec])
                    recip(REC[rows, :], REC[rows, :], [brec], [brec])
                    tt(YTA[rows, gp * 4:gp * 4 + 4, qt * 128:(qt + 1) * 128],
                       bky[rows, :].rearrange("p (a b) -> p a b", a=4),
                       REC[rows, :].rearrange("p (a b) -> p a b", a=4), ALU.mult, [bby, brec], [byta])

    def merge(which):
        bm = B("merged")
        for cg in range(8):
            si = nslab()
            sg = A.view(SLAB_OFF[si], BF16, [128, KC, 256])
            if which == "a":
                sw = A.view(SLAB_OFF[si] + 8192, BF16, [128, 8, 256])
                P.dma("pool", sg, wview(w_in, 0, KC, OFF_GA + cg * 256, 256), [], [b_slab[si]], sem_slab[si])
                wsrc = w_ab[:, cg * 256:(cg + 1) * 256].rearrange("(gp o i d) n -> o d gp i n", gp=2, o=2, i=4, d=64)
                for o in range(2):
                    for gp in range(2):
                        P.dma("pool", sw[o * 64:(o + 1) * 64, gp * 4:(gp + 1) * 4, :], wsrc[o][:, gp], [], [b_slab[si]], sem_slab[si])
            else:
                sw = A.view(SLAB_OFF[si] + 8192, BF16, [128, 8, 256])
                P.dma("pool", sg, wview(w_in, 0, KC, OFF_GB + cg * 256, 256), [], [b_slab[si]], sem_slab[si])
                P.dma("pool", sw, wview(w_hb, 0, 8, cg * 256, 256), [], [b_slab[si]], sem_slab[si])
            for j in range(2):
                c = cg * 2 + j
                for tb in range(2):
                    bkg, bbg = nbank()
                    for k in range(KC):
                        mm(bkg[:, :], sg[:, k, j * 128:(j + 1) * 128], HT[:, k, 128 + tb * 512:128 + (tb + 1) * 512],
                           k == 0, k == KC - 1, [b_slab[si]] + b_ht, [bbg])
                    bkw, bbw = nbank()
                    if which == "a":
                        for h in range(8):
                            mm(bkw[:, :], sw[:, h, j * 128:(j + 1) * 128], YTA[:, h, tb * 512:(tb + 1) * 512],
                               h == 0, h == 7, [b_slab[si], B("yta")], [bbw])
                    else:
                        for h in range(8):
                            mm(bkw[:, :], sw[:, h, j * 128:(j + 1) * 128], YTH[:, h, tb * 512:(tb + 1) * 512],
                               h == 0, h == 7, [b_slab[si], B("yth")], [bbw])
                    act(M_T, bkg[:, :], AF.Tanh, [bbg], [B("m_t")], scale=0.5)
                    dst = MERGED[:, c, tb * 512:(tb + 1) * 512]
                    if which == "a":
                        stt(dst, M_T, 1.0, bkw[:, :], ALU.add, ALU.mult, [B("m_t"), bbw], [bm])
                    else:
                        stt(M_M2, M_T, 1.0, bkw[:, :], ALU.add, ALU.mult, [B("m_t"), bbw], [B("m_m2")])
                        tt(dst, dst, M_M2, ALU.add, [bm, B("m_m2")], [bm])

    def hgrn_v(hf, full):
        bv = B("hv")
        if full:
            P.dma("sp", H_V, vc[hf], [B("c_v%d" % hf)], [bv], sem_cld["v"])
            return
        for cb in range(2):
            si = nslab()
            sv = slabview(si, 128, KC, 512)
            P.dma("pool", sv, wview(w_in, 0, KC, OFF_HI + cb * 512, 512), [], [b_slab[si]], sem_slab[si])
            for i in range(NT):
                bk, bb = nbank()
                for k in range(KC):
                    mm(bk[:, :], HT[:, k, 128 + i * 128:128 + (i + 1) * 128], sv[:, k, :], k == 0, k == KC - 1,
                       [b_slab[si], b_ht[i + 1]], [bb])
                cp(H_V[:, i, cb * 512:(cb + 1) * 512], bk[:, :], [bb], [bv], eng="act")
        P.dma("sp", vc[hf], H_V, [bv], [B("c_v%d" % hf)], sem_cst)

    def hg_A(hf, h, full):
        r = h % 2
        si = nslab()
        sv = slabview(si, 128, KC, 384)
        cols = (OFF_HQ, OFF_HG) if full else (OFF_HF,)
        for n, o in enumerate(cols):
            P.dma("pool", sv[:, :, n * 128:(n + 1) * 128], wview(w_in, 0, KC, o + h * 128, 128), [], [b_slab[si]], sem_slab[si])
        bth, bkk, bff, bp, bu = B("h_th"), B("h_kk"), B("h_ff"), B("h_p"), B("h_u")
        bkhb = B("h_khb")
        bkht, bpl = B("h_kht%d" % r), B("h_pl%d" % r)
        bcache = B("c_k%d_%d" % (hf, h))

        def proj(n, tb):
            bk, bb = nbank()
            for k in range(KC):
                mm(bk[:, :], sv[:, k, n * 128:(n + 1) * 128], HT[:, k, 128 + tb * 512:128 + (tb + 1) * 512],
                   k == 0, k == KC - 1, [b_slab[si]] + b_ht, [bb])
            return bk, bb
        if full:
            bktb, bqtb, bg2 = B("h_ktb%d" % r), B("h_qtb%d" % r), B("h_g2%d" % r)
            P.dma("sp", H_KTB[r], kc_ktb[hf, h], [bcache], [bktb], sem_cld["ktb%d" % r])
            P.dma("sp", H_KHT[r].rearrange("p a b -> p (a b)"), kc_kht[hf, h], [bcache], [bkht], sem_cld["kht%d" % r])
            P.dma("sp", H_P, kc_p[hf, h], [bcache], [bp], sem_cld["p"])
            cp(H_PL[:, r, :], H_P.rearrange("p (c t) -> p c t", t=64)[:, :, 63], [bp], [bpl])
            for tb in range(2):
                sl = slice(tb * 512, (tb + 1) * 512)
                bk, bb = proj(0, tb)
                act(H_TH[:, sl], bk[:, :], AF.Tanh, [bb], [bth], scale=0.5)
                stt(H_U[:, sl], H_TH[:, sl], 1.0, bk[:, :], ALU.add, ALU.mult, [bth, bb], [bu])
                stt(H_QTB[r][:, sl], H_U[:, sl], 0.5 / math.sqrt(128.0), H_P[:, sl], ALU.mult, ALU.mult, [bu, bp], [bqtb])
            for tb in range(2):
                sl = slice(tb * 512, (tb + 1) * 512)
                bk, bb = proj(1, tb)
                act(H_TH[:, sl], bk[:, :], AF.Tanh, [bb], [bth], scale=0.5)
                stt(H_G2[r][:, sl], H_TH[:, sl], 1.0, bk[:, :], ALU.add, ALU.mult, [bth, bb], [bg2])
        else:
            bktb = B("h_ktb0")
            for tb in range(2):
                sl = slice(tb * 512, (tb + 1) * 512)
                bk, bb = proj(0, tb)
                act(H_TH[:, sl], bk[:, :], AF.Tanh, [bb], [bth], scale=0.5)
                ts(H_KK[:, sl], H_TH[:, sl], c1[:, h:h + 1], c2[:, h:h + 1], ALU.mult, ALU.add, [bth, b_const], [bkk])
                ts(H_FF[:, sl], H_TH[:, sl], c2[:, h:h + 1], c3[:, h:h + 1], ALU.mult, ALU.add, [bth, b_const], [bff])
            for c in range(16):
                sl = slice(c * 64, (c + 1) * 64)
                P.op("dve", lambda e, sl=sl: e.tensor_tensor_scan(out=H_P[:, sl], data0=H_FF[:, sl], data1=zer[:, :64],
                                                                  initial=1.0, op0=ALU.mult, op1=ALU.add),
                     [bff, b_const], [bp])
            cp(H_PL[:, r, :], H_P.rearrange("p (c t) -> p c t", t=64)[:, :, 63], [bp], [bpl])
            recip(H_FF, H_P, [bp], [bff])
            tt(H_KK, H_KK, H_FF, ALU.mult, [bkk, bff], [bkk])
            cp(H_KTB0, H_KK, [bkk], [bktb], eng="act")
            tt(H_KHB.rearrange("p (c t) -> p c t", t=64), H_KK.rearrange("p (c t) -> p c t", t=64),
               H_PL[:, r, :].unsqueeze(2).to_broadcast([128, 16, 64]), ALU.mult, [bkk, bpl], [bkhb])
            bk, bb = nbank()
            bkb = bk[:, :].bitcast(BF16).rearrange("p (a b) -> p a b", a=8)
            for i in range(NT):
                tr(bkb[:, i, :], H_KHB[:, i * 128:(i + 1) * 128], ident, [bkhb, b_const], [bb], sig=(i == NT - 1))
            cp(H_KHT[r], bkb, [bb], [bkht], eng="act")
            P.dma("sp", kc_ktb[hf, h], H_KTB0, [bktb], [bcache], sem_cst)
            P.dma("sp", kc_kht[hf, h], H_KHT[r].rearrange("p a b -> p (a b)"), [bkht], [bcache], sem_cst)
            P.dma("sp", kc_p[hf, h], H_P, [bp], [bcache], sem_cst)

    def hg_B(hf, h, full):
        r = h % 2
        bkht, bpl = B("h_kht%d" % r), B("h_pl%d" % r)
        bS, bv = B("S"), B("hv")
        Sh = Sst[:, h, :]
        ring = [0, 0]
        if full:
            bktb, bqtb, bg2 = B("h_ktb%d" % r), B("h_qtb%d" % r), B("h_g2%d" % r)
        for tb in range(2):
            if full:
                bko, bbo = hbank()
            for tp in range(4):
                ti = tb * 4 + tp
                s128 = slice(ti * 128, (ti + 1) * 128)
                vt = H_V[:, ti, h * 128:(h + 1) * 128]
                if full:
                    bka, bba = nbank()
                    mm(bka[:, :128], H_KTB[r][:, s128], H_QTB[r][:, s128], True, True, [bktb, bqtb], [bba])
                    ai = ring[1] % 2
                    ring[1] += 1
                    bAm = B("h_am%d" % ai)
                    tt(H_AM[:, ai, :], bka[:, :128], caus, ALU.mult, [bba, b_const], [bAm])
                    mm(bko[:, tp * 128:(tp + 1) * 128], vt, H_AM[:, ai, :], True, False, [bv, bAm], [bbo], sig=False)
                for cc in range(2):
                    c = ti * 2 + cc
                    sl = slice(c * 64, (c + 1) * 64)
                    po = cc * 64
                    if full:
                        sbi = ring[0] % 4
                        ring[0] += 1
                        bSb = B("h_sb%d" % sbi)
                        cp(H_SB[:, sbi, :], Sh, [bS], [bSb], eng="act")
                        mm(bko[:, tp * 128 + po:tp * 128 + po + 64], H_SB[:, sbi, :], H_QTB[r][:, sl], False, cc == 1,
                           [bSb, bqtb], [bbo], sig=(cc == 1))
                    bks, bbs = nbank()
                    mm(bks[:, :128], H_KHT[r][po:po + 64, ti, :], H_V[po:po + 64, ti, h * 128:(h + 1) * 128], True, True,
                       [bkht, bv], [bbs])
                    stt(Sh, Sh, H_PL[:, r, c:c + 1], bks[:, :128], ALU.mult, ALU.add, [bS, bpl, bbs], [bS])
                    if not full:
                        tt(dtot[:, h:h + 1], dtot[:, h:h + 1], H_PL[:, r, c:c + 1], ALU.mult,
                           [B("dtot"), bpl], [B("dtot")])
            if full:
                sl = slice(tb * 512, (tb + 1) * 512)
                act(H_SQ, bko[:, :], AF.Square, [bbo], [B("h_sq")])
                cp(H_OSB, bko[:, :], [bbo, B("h_sq")], [B("h_osb")])
                bkn, bbn = nbank()
                mm(bkn[:, :], ones, H_SQ, True, True, [b_const, B("h_sq")], [bbn])
                act(H_RS, bkn[:, :], AF.Sqrt, [bbn, b_const], [B("h_rs")], bias=epsc, scale=1.0 / 128.0)
                recip(H_RS, H_RS, [B("h_rs")], [B("h_rs")])
                tt(H_OSB, H_OSB, H_RS, ALU.mult, [B("h_osb"), B("h_rs")], [B("h_osb")])
                stt(YTH[:, h, sl], H_OSB, gn05, H_G2[r][:, sl], ALU.mult, ALU.mult, [B("h_osb"), b_const, bg2], [B("yth")])

    def hgrn_all(hf, full):
        hg_A(hf, 0, full)
        for h in range(8):
            if h + 1 < 8:
                hg_A(hf, h + 1, full)
            hg_B(hf, h, full)

    def exchange_state():
        bS = B("S")
        bb = B("bnc")
        P.dma("pool", bnc[0:1024, :].rearrange("(h k) v -> k h v", k=128), Sst, [bS], [bb], sem_pmisc)
        bgd = B("gu")
        P.op("dve", lambda e: e.memset(H_GD, 0.0), [], [bgd])
        cp(H_GD[:, 0:8], dtot, [B("dtot"), bgd], [bgd])
        P.dma("pool", bnc[1024:1152, :], H_GD, [bgd], [bb], sem_pmisc)
        w = P._deps("pool", [bb], [B("gth")])
        csem = P.dsem("cc")

        def cc(e):
            return e.collective_compute("AllGather", ALU.bypass, replica_groups=[list(range(NCORES))],
                                        ins=[bnc.ap().opt()], outs=[gth.ap().opt()])
        P.dcnt[csem] += 1
        tok = (csem, P.dcnt[csem])
        P._mark(tok, [bb], [B("gth")])
        P.q["pool"].append((w, cc, ("raw", csem)))
        P.op("dve", lambda e: e.memset(Sst, 0.0), [], [bS])
        for r in range(NCORES):
            bgu = B("gu")
            P.dma("sp", H_GU, gth[r * GROWS:r * GROWS + 1024, :].rearrange("(h k) v -> k h v", k=128), [B("gth")], [bgu], sem_misc)
            P.dma("sp", H_GD[:, 0:8], gth[r * GROWS + 1024:r * GROWS + 1152, 0:8], [B("gth")], [bgu], sem_misc)
            ts(H_GD[:, 0:8], H_GD[:, 0:8], sel[:, r:r + 1], dsel[:, r:r + 1], ALU.mult, ALU.add, [bgu, b_const], [bgu])
            ts(H_GU, H_GU, sel[:, r:r + 1], None, ALU.mult, None, [bgu, b_const], [bgu])
            for h in range(8):
                stt(Sst[:, h, :], Sst[:, h, :], H_GD[:, h:h + 1], H_GU[:, h, :], ALU.mult, ALU.add, [bS, bgu], [bS])

    def residual(src_rows, dst_rows, gain_ap):
        gi = 1
        P.dma("sp", GAIN[gi], gain_ap, [], [b_gain[gi]], sem_gain[gi])
        bsr = B("ssr")
        for i in range(NT):
            act(JUNK, OUT2[:, i, :], AF.Square, [b_o2[i]], [b_junk, bsr], accum_out=ssr[:, i:i + 1])
        act(rsr, ssr, AF.Sqrt, [bsr, b_const], [bsr], bias=epsc, scale=1.0 / (4.0 * D))
        recip(rsr, rsr, [bsr], [bsr])
        ts(rsr, rsr, 0.5, None, ALU.mult, None, [bsr], [bsr])
        for i in range(NT):
            s = i % 2
            P.dma("sp", XT[s], src_rows(i), [], [b_xt[s]], sem_xt[s])
            stt(OUT2[:, i, :], OUT2[:, i, :], rsr[:, i:i + 1], GAIN[gi], ALU.mult, ALU.mult, [b_o2[i], bsr, b_gain[gi]], [b_o2[i]])
            tt(XT[s], XT[s], OUT2[:, i, :], ALU.add, [b_xt[s], b_o2[i]], [b_xt[s]])
            P.dma("sp", dst_rows(i), XT[s], [b_xt[s]], [B("dram_x")], sem_st[s])

    def ffn(hf):
        for qd in range(4):
            bh = B("hid")
            fc = 0
            while fc < 11:
                G = min(2, 11 - fc)
                f0 = qd * 11 + fc
                si = nslab()
                sg = A.view(SLAB_OFF[si], BF16, [128, KC, 256])
                su = A.view(SLAB_OFF[si] + 8192, BF16, [128, KC, 256])
                P.dma("pool", sg[:, :, :G * 128], wview(w_gu, 0, KC, f0 * 128, G * 128), [], [b_slab[si]], sem_slab[si])
                P.dma("pool", su[:, :, :G * 128], wview(w_gu, 0, KC, DFF + f0 * 128, G * 128), [], [b_slab[si]], sem_slab[si])
                for j in range(G):
                    for tb in range(2):
                        sl = slice(tb * 512, (tb + 1) * 512)
                        bkg, bbg = nbank()
                        for k in range(KC):
                            mm(bkg[:, :], sg[:, k, j * 128:(j + 1) * 128], HT[:, k, sl], k == 0, k == KC - 1,
                               [b_slab[si]] + b_ht, [bbg])
                        bku, bbu = nbank()
                        for k in range(KC):
                            mm(bku[:, :], su[:, k, j * 128:(j + 1) * 128], HT[:, k, sl], k == 0, k == KC - 1,
                               [b_slab[si]] + b_ht, [bbu])
                        act(M_T, bkg[:, :], AF.Tanh, [bbg], [B("m_t")], scale=0.5)
                        stt(M_M2, M_T, 1.0, bkg[:, :], ALU.add, ALU.mult, [B("m_t"), bbg], [B("m_m2")])
                        tt(HID[:, fc + j, sl], M_M2, bku[:, :], ALU.mult, [B("m_m2"), bbu], [bh])
                fc += G

            def evac(cb, i, bk, bb, qd=qd):
                dst = OUT2[:, i, cb * 512:(cb + 1) * 512]
                if qd == 0:
                    cp(dst, bk[:, :], [bb], [b_o2[i]], eng="act")
                else:
                    tt(dst, dst, bk[:, :], ALU.add, [b_o2[i], bb], [b_o2[i]])
            lin_tm(HID, lambda i: [bh], 11, lambda cb, qd=qd: wview(w_down, qd * 1408, 11, cb * 512, 512), evac)

    def ple(hf):
        bpt = B("pt")
        for i in range(NT):
            s = i % 2
            P.dma("sp", XT[s][:, :256], pin[hf * T + i * 128:hf * T + (i + 1) * 128, :], [], [b_xt[s]], sem_xt[s])
            cp(XNB[s][:, :256], XT[s][:, :256], [b_xt[s]], [b_xnb[s]])
            bk, bb = nbank()
            bkb = bk[:, :].bitcast(BF16).rearrange("p (a b) -> p a b", a=8)
            for j in range(2):
                tr(bkb[:, j, :], XNB[s][:, j * 128:(j + 1) * 128], ident, [b_xnb[s], b_const], [bb], sig=(j == 1))
            cp(PT[:, :, i * 128:(i + 1) * 128], bkb[:, 0:2, :], [bb], [bpt], eng="act")
        bwp = B("wpp")
        P.dma("pool", WPP, wview(w_pp, 0, 2, 0, D), [], [bwp], sem_pmisc)
        for cb in range(4):
            si = nslab()
            sv = slabview(si, 128, KC, 512)
            P.dma("pool", sv, wview(w_pg, 0, KC, cb * 512, 512), [], [b_slab[si]], sem_slab[si])
            for i in range(NT):
                bkg, bbg = nbank()
                for k in range(KC):
                    mm(bkg[:, :], HT[:, k, i * 128:(i + 1) * 128], sv[:, k, :], k == 0, k == KC - 1, [b_slab[si], b_ht[i]], [bbg])
                bkp, bbp = nbank()
                for k in range(2):
                    mm(bkp[:, :], PT[:, k, i * 128:(i + 1) * 128], WPP[:, k, cb * 512:(cb + 1) * 512], k == 0, k == 1, [bwp, bpt], [bbp])
                act(M_T, bkg[:, :], AF.Tanh, [bbg], [B("m_t")], scale=0.5)
                stt(OUT2[:, i, cb * 512:(cb + 1) * 512], M_T, 1.0, bkp[:, :], ALU.add, ALU.mult, [B("m_t"), bbp], [b_o2[i]])

    def dump(src, npart, ncol, row0=0):
        P.barrier()
        stg = A.view(SLAB_OFF[0], F32, [128, 4096])
        bd = B("dump")
        done = 0
        while done < ncol:
            n = min(2048, ncol - done)
            cp(stg[:npart, :n], src[:npart, done:done + n], [], [bd])
            P.dma("sp", yout[row0:row0 + npart, done:done + n], stg[:npart, :n], [bd], [B("dram_x")], sem_misc)
            P.barrier()
            done += n

    def program():
        setup()
        P.barrier()
        if stop == "setup":
            dump(mcur, 128, 512, 0)
            dump(smallf, 128, 128, 128)
            dump(caus, 128, 64, 256)
            return
        for hf in range(0 if var == 3 else 2):
            norm_T(lambda i, hf=hf: xh[128 + hf * T + i * 128:128 + hf * T + (i + 1) * 128, :], NT, g_mix_pre, HT, b_ht, tile0=1)
            P.dma("sp", hTc[hf], HT[:, :, 128:TH], b_ht[1:], [B("c_ht%d" % hf)], sem_cst)
            if stop == "prenorm":
                for k in range(KC):
                    dump(HT[:, k, :], 128, TH, k * 128)
                return
            hgrn_v(hf, False)
            if stop == "prev":
                for i in range(NT):
                    dump(H_V[:, i, :], 128, 1024, i * 128)
                return
            hgrn_all(hf, False)
            P.barrier()
        if stop == "prephase":
            dump(Sst.rearrange("p a b -> p (a b)"), 128, 1024, 0)
            dump(smallf, 128, 128, 128)
            return
        if not skipx:
            exchange_state()
        else:
            P.op("dve", lambda e: e.memset(Sst, 0.0), [], [B("S")])
        P.barrier()
        if stop == "exchange":
            dump(Sst.rearrange("p a b -> p (a b)"), 128, 1024, 0)
            return
        for hf in range(2):
            if var != 1:
                rope_tables(hf)
            P.barrier()
            if stop == "rope":
                dump(cosT, 128, TH, 0)
                dump(sinT, 128, TH, 128)
                return
            if hf == 0:
                norm_T(lambda i: xh[i * 128:(i + 1) * 128, :], 1, g_mix_pre, HT, b_ht)
            else:
                P.dma("sp", HT[:, :, 0:128], hTc[0][:, :, T - 128:T], [B("c_ht0")], [b_ht[0]], sem_cld["ht0"])
            P.dma("sp", HT[:, :, 128:TH], hTc[hf], [B("c_ht%d" % hf)], b_ht[1:], sem_cld["ht"])
            if stop == "norm9":
                for k in range(KC):
                    dump(HT[:, k, :], 128, TH, k * 128)
                return
            attention(hf)
            if stop in ("attn", "attn_k", "attn_q"):
                for g in range(2):
                    dump(KT[:, g, :], 128, TH, g * 128)
                if stop == "attn_k":
                    for i in range(NTH):
                        dump(VTM[:, i, :], 128, 256, 256 + i * 128)
                    return
                for h in range(8):
                    dump(YTA[:, h, :], 128, T, 256 + h * 128)
                if stop == "attn_q":
                    return
                for h in range(4):
                    dump(QT[:, h, :], 128, T, 1280 + h * 128)
                return
            merge("a")
            P.barrier()
            if stop == "mergea":
                for k in range(KC):
                    dump(MERGED[:, k, :], 128, T, k * 128)
                return
            hgrn_v(hf, True)
            hgrn_all(hf, True)
            if stop == "hgrn":
                for h in range(8):
                    dump(YTH[:, h, :], 128, T, h * 128)
                return
            merge("b")
            P.barrier()
            if stop == "mergeb":
                for k in range(KC):
                    dump(MERGED[:, k, :], 128, T, k * 128)
                return

            def evac_copy(cb, i, bk, bb):
                cp(OUT2[:, i, cb * 512:(cb + 1) * 512], bk[:, :], [bb], [b_o2[i]], eng="act")
            lin_tm(MERGED, lambda i: [B("merged")], KC, lambda cb: wview(w_out, 0, KC, cb * 512, 512), evac_copy)
            P.barrier()
            residual(lambda i, hf=hf: xh[128 + hf * T + i * 128:128 + hf * T + (i + 1) * 128, :],
                     lambda i, hf=hf: x1d[hf * T + i * 128:hf * T + (i + 1) * 128, :], g_mix_post)
            P.barrier()
            if stop == "x1":
                for i in range(NT):
                    P.dma("sp", XT[0], x1d[i * 128:(i + 1) * 128, :], [], [b_xt[0]], sem_xt[0])
                    P.dma("sp", yout[i * 128:(i + 1) * 128, :], XT[0], [b_xt[0]], [B("dram_x")], sem_st[0])
                P.barrier()
                return
            norm_T(lambda i, hf=hf: x1d[hf * T + i * 128:hf * T + (i + 1) * 128, :], NT, g_ffn_pre, HT, b_ht)
            P.barrier()
            ffn(hf)
            P.barrier()
            residual(lambda i, hf=hf: x1d[hf * T + i * 128:hf * T + (i + 1) * 128, :],
                     lambda i, hf=hf: x2d[hf * T + i * 128:hf * T + (i + 1) * 128, :], g_ffn_post)
            P.barrier()
            if stop == "x2":
                for i in range(NT):
                    P.dma("sp", XT[0], x2d[i * 128:(i + 1) * 128, :], [], [b_xt[0]], sem_xt[0])
                    P.dma("sp", yout[i * 128:(i + 1) * 128, :], XT[0], [b_xt[0]], [B("dram_x")], sem_st[0])
                P.barrier()
                return
            norm_T(lambda i, hf=hf: x2d[hf * T + i * 128:hf * T + (i + 1) * 128, :], NT, g_ple_pre, HT, b_ht)
            P.barrier()
            ple(hf)
            P.barrier()
            residual(lambda i, hf=hf: x2d[hf * T + i * 128:hf * T + (i + 1) * 128, :],
                     lambda i, hf=hf: yout[hf * T + i * 128:hf * T + (i + 1) * 128, :], g_ple_post)
            P.barrier()

    program()
    with nc.Block() as block:
        P.replay(block)
    es.close()
    return nc


_NC_CACHE = {}
_DEBUG = {}


def _consts():
    c = {}
    c["c_invf"] = (10000.0 ** (-(np.arange(128) % 32).astype(np.float32) / 32.0)).astype(np.float32).reshape(128, 1)
    c["c_sgn"] = np.where((np.arange(128) % 64) < 32, -1.0, 1.0).astype(np.float32).reshape(128, 1)
    c["c_ident"] = np.eye(128, dtype=np.float32)
    c["c_ones"] = np.ones((128, 128), np.float32)
    pm = np.zeros((128, 128), np.float32)
    for m in range(128):
        pm[(m // 64) * 64 + (m % 64 + 32) % 64, m] = 1.0
    c["c_perm"] = pm
    k = np.arange(128)[:, None]
    q = np.arange(128)[None, :]
    c["c_mcur"] = np.tile((k <= q).astype(np.float32), (1, 4))
    c["c_mprevB"] = np.tile((k > q).astype(np.float32), (1, 4))
    s = np.arange(128)[:, None]
    t = np.arange(128)[None, :]
    c["c_caus"] = ((s <= t) & (s // 64 == t // 64)).astype(np.float32)
    return c


def kernel(x, p, positions, g_mix_pre, w_in, attn_sinks, hgrn_lb_logits, hgrn_gnorm,
           w_attn_branch, w_hgrn_branch, w_out, g_mix_post, g_ffn_pre, w_gate_up, w_down,
           g_ffn_post, g_ple_pre, w_ple_gate, w_ple_proj, g_ple_post):
    f = lambda a: np.ascontiguousarray(np.asarray(a), dtype=np.float32)
    x = f(x)
    p = f(p)
    positions = np.ascontiguousarray(np.asarray(positions), dtype=np.int32)
    cst = _consts()

    def bc(g):
        return np.ascontiguousarray(np.broadcast_to(f(g)[0][None, :], (128, D)))
    shared = {
        "g_mix_pre": bc(g_mix_pre), "g_mix_post": bc(g_mix_post), "g_ffn_pre": bc(g_ffn_pre),
        "g_ffn_post": bc(g_ffn_post), "g_ple_pre": bc(g_ple_pre), "g_ple_post": bc(g_ple_post),
        "w_in": f(w_in)[0], "w_ab": f(w_attn_branch)[0], "w_hb": f(w_hgrn_branch)[0], "w_out": f(w_out)[0],
        "w_gu": f(w_gate_up)[0], "w_down": f(w_down)[0], "w_pg": f(w_ple_gate)[0], "w_pp": f(w_ple_proj)[0],
        "sink_rep": np.ascontiguousarray(np.broadcast_to(f(attn_sinks)[0][None, :], (128, 16))),
        "lbl": np.ascontiguousarray(np.concatenate([f(hgrn_lb_logits)[0].reshape(8, 128).T,
                                                    f(hgrn_lb_logits)[1].reshape(8, 128).T], axis=1)),
        "gnc": np.ascontiguousarray(f(hgrn_gnorm)[0].reshape(128, 1)),
    }
    shared.update({k: v for k, v in cst.items()})
    in_maps = []
    for c in range(NCORES):
        b, j = c // 4, c % 4
        t0 = j * SEG
        xhh = np.zeros((SEG + 128, D), np.float32)
        xhh[128:] = x[b, t0:t0 + SEG]
        posh = np.zeros((SEG + 128,), np.int32)
        posh[128:] = positions[b, t0:t0 + SEG]
        if j > 0:
            xhh[:128] = x[b, t0 - 128:t0]
            posh[:128] = positions[b, t0 - 128:t0]
        m = dict(shared)
        m["xh"] = xhh
        m["p"] = np.ascontiguousarray(p[0, b, t0:t0 + SEG])
        m["posr"] = np.ascontiguousarray(np.broadcast_to(posh[None, :], (128, SEG + 128)))
        m["c_mprevA"] = cst["c_mprevB"] if j > 0 else np.zeros((128, 512), np.float32)
        selv = np.zeros((NCORES,), np.float32)
        for r in range(NCORES):
            if r // 4 == b and r % 4 < j:
                selv[r] = 1.0
        m["c_sel"] = np.ascontiguousarray(np.broadcast_to(selv[None, :], (128, NCORES)))
        in_maps.append(m)
    if _DEBUG.get("maps_only"):
        return in_maps
    if "nc" not in _NC_CACHE:
        _NC_CACHE["nc"] = build_program()
    nc = _NC_CACHE["nc"]
    res = run_bass_kernel_spmd(nc, in_maps, core_ids=list(range(NCORES)))
    out = np.empty((2, 4 * SEG, D), np.float32)
    for c in range(NCORES):
        b, j = c // 4, c % 4
        out[b, j * SEG:(j + 1) * SEG] = res.results[c]["y"]
    return out
```
